# Optimizing a Trainium2 kernel written in Bass

```python
import jax, jax.numpy as jnp
from jax import lax
import numpy as np

D_MODEL = 2048
BATCH = 1
SEQ = 16384
DEPTH = 2
DEC_BATCH = 8
DEC_SEQ = 16
PAST_LEN = 1024

CHUNK = 64
N_MIXERS = 2
N_SG_LAYERS = (DEPTH + 1) // 2
N_HG_LAYERS = DEPTH // 2
D_FF = 5632
SG_CHUNK = 128
D_SG = D_MODEL
SG_GROUPS = 16
SG_GROUP_DIM = D_SG // SG_GROUPS
HG_HEADS = 16
HG_DK = D_MODEL // HG_HEADS
HG_DV = D_MODEL // HG_HEADS
EPS = 1e-6

kernel_name = "hybrid_sgmlp_hgrn2_streaming_step"


def rms_norm(x, g):
    xf = x.astype(jnp.float32)
    y = xf * lax.rsqrt(jnp.mean(xf * xf, axis=-1, keepdims=True) + EPS)
    return (y * g.astype(jnp.float32)).astype(x.dtype)


def layer_norm(x, g, b):
    xf = x.astype(jnp.float32)
    mu = jnp.mean(xf, axis=-1, keepdims=True)
    var = jnp.mean(jnp.square(xf - mu), axis=-1, keepdims=True)
    y = (xf - mu) * lax.rsqrt(var + EPS)
    return (y * g.astype(jnp.float32) + b.astype(jnp.float32)).astype(x.dtype)


def swiglu_ffn(x, w_up, w_down):
    a, b = jnp.split(x @ w_up, 2, axis=-1)
    return (jax.nn.silu(a) * b) @ w_down


def spatial_gating_mixer(h, w_in, ln_g, ln_b, w_s, b_s, w_out):
    B, T, _ = h.shape
    u, v = jnp.split(jax.nn.gelu(h @ w_in, approximate=False), 2, axis=-1)
    v = layer_norm(v, ln_g, ln_b)
    L = min(T, SG_CHUNK)
    n = T // L
    blk = jnp.arange(SG_CHUNK) // CHUNK
    mask = (blk[:, None] >= blk[None, :])[:L, :L]
    w = jnp.where(mask[None], w_s[:, :L, :L], jnp.zeros((), w_s.dtype))
    vg = v.reshape(B, n, L, SG_GROUPS, SG_GROUP_DIM)
    s = jnp.einsum('gij,bnjgd->bnigd', w, vg) + b_s[:, :L].T[None, None, :, :, None]
    y = u * s.reshape(B, T, D_SG)
    return y @ w_out, v


def layer_lower_bound(hg_lower, layer_idx):
    lbs = jnp.cumsum(jax.nn.softmax(hg_lower.astype(jnp.float32), axis=0), axis=0)
    lbs = lbs - lbs[0]
    return lbs[layer_idx]


def hgrn_blocked_scan(S0, q, k, v, logf, block):
    B, T, H, K = q.shape
    n = T // block

    def to_blocks(a):
        return a.reshape(B, n, block, H, a.shape[-1]).swapaxes(0, 1)

    tril = jnp.tril(jnp.ones((block, block), bool))[None, :, :, None, None]

    def step(S, xs):
        qb, kb, vb, lfb = xs
        b = jnp.cumsum(lfb, axis=1)
        inter = jnp.einsum('bthk,bhkv->bthv', qb * jnp.exp(b), S)
        decay = jnp.exp(jnp.where(tril, b[:, :, None] - b[:, None], -jnp.inf))
        scores = jnp.einsum('bthk,btshk,bshk->btsh', qb, decay, kb)
        intra = jnp.einsum('btsh,bshv->bthv', scores, vb)
        bl = b[:, -1]
        S_new = jnp.exp(bl)[..., None] * S + jnp.einsum('bshk,bshv->bhkv', kb * jnp.exp(bl[:, None] - b), vb)
        return S_new, inter + intra

    S, o = lax.scan(step, S0, (to_blocks(q), to_blocks(k), to_blocks(v), to_blocks(logf)))
    return S, o.swapaxes(0, 1).reshape(B, T, H, v.shape[-1])


def hgrn2_mixer(h, w_in, lb, norm_g, w_out, S0, block):
    B, T, _ = h.shape
    q, fz, i_, gz = jnp.split(h @ w_in, 4, axis=-1)
    q = jax.nn.silu(q.astype(jnp.float32)).reshape(B, T, HG_HEADS, HG_DK)
    f = lb + (1.0 - lb) * jax.nn.sigmoid(fz.astype(jnp.float32))
    logf = jnp.log(f).reshape(B, T, HG_HEADS, HG_DK)
    k = (1.0 - f).reshape(B, T, HG_HEADS, HG_DK)
    v = i_.astype(jnp.float32).reshape(B, T, HG_HEADS, HG_DV)
    S, o = hgrn_blocked_scan(S0, q, k, v, logf, block)
    gate = jax.nn.silu(gz.astype(jnp.float32)).reshape(B, T, HG_HEADS, HG_DV)
    o = rms_norm(o, norm_g) * gate
    return o.reshape(B, T, D_MODEL).astype(h.dtype) @ w_out, S


def setup_inputs(seed: int = 0) -> dict:
    key = jax.random.key(seed)
    ks = jax.random.split(key, 16)

    def nrm(k, shape, scale):
        return jax.random.normal(k, shape, jnp.float32) * scale

    return {
        "x_prompt": nrm(ks[0], (BATCH, SEQ, D_MODEL), 1.0),
        "x_sample": nrm(ks[1], (DEC_BATCH, DEC_SEQ, D_MODEL), 1.0),
        "state_hgrn": nrm(ks[2], (N_HG_LAYERS, DEC_BATCH, HG_HEADS, HG_DK, HG_DV), 0.5),
        "norm_g": 1.0 + nrm(ks[3], (DEPTH, 6, D_MODEL), 0.05),
        "ffn_w_up": nrm(ks[4], (DEPTH, 2, D_MODEL, 2 * D_FF), D_MODEL ** -0.5),
        "ffn_w_down": nrm(ks[5], (DEPTH, 2, D_FF, D_MODEL), D_FF ** -0.5),
        "sg_w_in": nrm(ks[6], (N_SG_LAYERS, D_MODEL, 2 * D_SG), D_MODEL ** -0.5),
        "sg_ln_g": 1.0 + nrm(ks[7], (N_SG_LAYERS, D_SG), 0.05),
        "sg_ln_b": nrm(ks[8], (N_SG_LAYERS, D_SG), 0.02),
        "sg_w_s": nrm(ks[9], (N_SG_LAYERS, SG_GROUPS, SG_CHUNK, SG_CHUNK), SG_CHUNK ** -0.5),
        "sg_b_s": 1.0 + nrm(ks[10], (N_SG_LAYERS, SG_GROUPS, SG_CHUNK), 0.02),
        "sg_w_out": nrm(ks[11], (N_SG_LAYERS, D_SG, D_MODEL), D_SG ** -0.5),
        "hg_w_in": nrm(ks[12], (N_HG_LAYERS, D_MODEL, 4 * D_MODEL), D_MODEL ** -0.5),
        "hg_lower": nrm(ks[13], (DEPTH, D_MODEL), 0.1),
        "hg_norm_g": 1.0 + nrm(ks[14], (N_HG_LAYERS, HG_DV), 0.05),
        "hg_w_out": nrm(ks[15], (N_HG_LAYERS, D_MODEL, D_MODEL), D_MODEL ** -0.5),
    }


def reference(x_prompt, x_sample, state_hgrn, norm_g, ffn_w_up, ffn_w_down,
              sg_w_in, sg_ln_g, sg_ln_b, sg_w_s, sg_b_s, sg_w_out,
              hg_w_in, hg_lower, hg_norm_g, hg_w_out):

    def trunk(x, hg_init, block):
        hg_states, sg_vs = [], []
        for i in range(DEPTH):
            g = norm_g[i]
            j = i // N_MIXERS
            x = x + 0.5 * rms_norm(swiglu_ffn(rms_norm(x, g[0]), ffn_w_up[i, 0], ffn_w_down[i, 0]), g[1])
            h = rms_norm(x, g[2])
            if i % N_MIXERS == 0:
                m, v_rows = spatial_gating_mixer(h, sg_w_in[j], sg_ln_g[j], sg_ln_b[j],
                                                 sg_w_s[j], sg_b_s[j], sg_w_out[j])
                sg_vs.append(v_rows)
            else:
                lb = layer_lower_bound(hg_lower, i)
                m, S = hgrn2_mixer(h, hg_w_in[j], lb, hg_norm_g[j], hg_w_out[j], hg_init[j], block)
                hg_states.append(S)
            x = x + rms_norm(m, g[3])
            x = x + 0.5 * rms_norm(swiglu_ffn(rms_norm(x, g[4]), ffn_w_up[i, 1], ffn_w_down[i, 1]), g[5])
        return x, hg_states, sg_vs

    prompt_init = [jnp.zeros((x_prompt.shape[0], HG_HEADS, HG_DK, HG_DV), jnp.float32)
                   for _ in range(N_HG_LAYERS)]
    y_prompt, hg_p, _ = trunk(x_prompt, prompt_init, CHUNK)

    sample_init = [state_hgrn[j].astype(jnp.float32) for j in range(N_HG_LAYERS)]
    y_sample, hg_s, sg_v_s = trunk(x_sample, sample_init, x_sample.shape[1])

    state_hgrn_prompt = jnp.stack(hg_p, axis=0)
    state_hgrn_sample = jnp.stack(hg_s, axis=0)
    state_sg_v_sample = jnp.stack(sg_v_s, axis=0)
    return (y_prompt, y_sample, state_hgrn_prompt, state_hgrn_sample, state_sg_v_sample)
```

```python
import numpy as np
import concourse.bass as bass
import concourse.mybir as mybir
from concourse.bass_utils import run_bass_kernel_spmd

F32 = mybir.dt.float32
BF16 = mybir.dt.bfloat16
AF = mybir.ActivationFunctionType
ALU = mybir.AluOpType

D = 2048
DFF = 5632
NCH = 16
EPS = 1e-6
HALO = 128
OWN = 2048
NSAMP = 16
TOT = HALO + OWN + NSAMP
NTM = 384
TILES = [(0, 384, 0), (384, 384, 0), (768, 384, 0), (1152, 384, 0), (1536, 384, 0), (1920, 256, 16)]
NWB = 5
SPILL = False
SLABC = 256
BLK = 256
ENGS = ("pe", "act", "dve", "pool", "sp")


def _esz(dt):
    return 4 if dt == F32 else 2


class Builder:
    def __init__(self, nc, n_tiles=None, stages=None):
        self.nc = nc
        self.stages = stages or ["ffn00", "sg", "ffn01", "ffn10", "hg", "ffn11"]
        self.streams = {e: [] for e in ENGS}
        self.cnt = {e: 0 for e in ENGS}
        self.known = {e: {} for e in ENGS}
        self.lastw = {}
        self.readers = {}
        self.sems = {}
        self.dmacnt = {}
        self.tiles = TILES if n_tiles is None else TILES[:n_tiles]

    def blocks(self, ap):
        t = ap.tensor
        if type(t).__name__.startswith("DRam"):
            return ()
        es = _esz(t.dtype)
        name = t.name
        dims = [(s_, c_) for s_, c_ in ap.ap[1:] if s_ != 0 and c_ != 1]
        inner = 1
        if dims and dims[-1][0] == 1:
            inner = dims[-1][1]
            dims = dims[:-1]
        offs = [ap.offset]
        for s_, c_ in dims:
            offs = [o + i * s_ for o in offs for i in range(c_)]
        out = set()
        for o in offs:
            b0 = (o * es) // BLK
            b1 = ((o + inner) * es - 1) // BLK
            for b in range(b0, b1 + 1):
                out.add((name, b))
        return out

    def op(self, eng, fn, reads=(), writes=(), signal=True, dma_sem=None):
        deps = {}

        def add(tok):
            s, v = tok
            if deps.get(s, 0) < v:
                deps[s] = v
        rb = set()
        for a in reads:
            rb |= set(self.blocks(a))
        wb = set()
        for a in writes:
            wb |= set(self.blocks(a))
        for k in rb:
            t = self.lastw.get(k)
            if t is not None:
                add(t)
        for k in wb:
            t = self.lastw.get(k)
            if t is not None:
                add(t)
            for s, v in self.readers.get(k, {}).items():
                add((s, v))
        my = self.sems[eng]
        for s, v in deps.items():
            if s is my and (eng in ("pe", "sp") or dma_sem is not None):
                continue
            if s is my and dma_sem is None and v > self.cnt[eng]:
                continue
            if self.known[eng].get(id(s), 0) >= v:
                continue
            self.known[eng][id(s)] = v
            self.streams[eng].append(("wait", s, v))
        if dma_sem is not None:
            self.dmacnt[id(dma_sem)] = self.dmacnt.get(id(dma_sem), 0) + 16
            tok = (dma_sem, self.dmacnt[id(dma_sem)])
            self.streams[eng].append(("op", fn, dma_sem, 16))
        elif signal:
            self.cnt[eng] += 1
            tok = (my, self.cnt[eng])
            self.streams[eng].append(("op", fn, my, 1))
        else:
            tok = (my, self.cnt[eng] + 1)
            self.streams[eng].append(("op", fn, None, 0))
        for k in rb:
            d = self.readers.setdefault(k, {})
            if d.get(tok[0], 0) < tok[1]:
                d[tok[0]] = tok[1]
        for k in wb:
            self.lastw[k] = tok
            self.readers[k] = {}
        return tok

    def replay(self, eng, e):
        for item in self.streams[eng]:
            if item[0] == "wait":
                e.wait_ge(item[1], item[2])
            else:
                ins = item[1](e)
                if item[2] is not None:
                    ins.then_inc(item[2], item[3])

    def mm(self, out, lhsT, rhs, start, stop, signal=None):
        if signal is None:
            signal = stop
        self.op("pe", lambda e: e.matmul(out, lhsT, rhs, start=start, stop=stop),
                reads=[lhsT, rhs], writes=[out], signal=signal)

    def act(self, out, in_, func, scale=None, bias=None):
        kw = {}
        if scale is not None:
            kw["scale"] = scale
        if bias is not None:
            kw["bias"] = bias
        rd = [in_] + [x for x in (scale, bias) if x is not None and not isinstance(x, float)]
        self.op("act", lambda e: e.activation(out=out, in_=in_, func=func, **kw), reads=rd, writes=[out])

    def amul(self, out, in_, mul):
        rd = [in_] + ([] if isinstance(mul, float) else [mul])
        self.op("act", lambda e: e.mul(out, in_, mul), reads=rd, writes=[out])

    def acopy(self, out, in_):
        self.op("act", lambda e: e.copy(out, in_), reads=[in_], writes=[out])

    def tt(self, out, in0, in1, op, eng="dve"):
        self.op(eng, lambda e: e.tensor_tensor(out=out, in0=in0, in1=in1, op=op), reads=[in0, in1], writes=[out])

    def ts(self, out, in0, s1, s2, op0, op1=None, eng="dve"):
        rd = [in0] + [x for x in (s1, s2) if x is not None and not isinstance(x, float)]
        if op1 is None:
            self.op(eng, lambda e: e.tensor_scalar(out=out, in0=in0, scalar1=s1, scalar2=None, op0=op0),
                    reads=rd, writes=[out])
        else:
            self.op(eng, lambda e: e.tensor_scalar(out=out, in0=in0, scalar1=s1, scalar2=s2, op0=op0, op1=op1),
                    reads=rd, writes=[out])

    def stt(self, out, in0, scalar, in1, op0, op1):
        rd = [in0, in1] + ([] if isinstance(scalar, float) else [scalar])
        self.op("dve", lambda e: e.scalar_tensor_tensor(out=out, in0=in0, scalar=scalar, in1=in1, op0=op0, op1=op1),
                reads=rd, writes=[out])

    def vcopy(self, out, in_):
        self.op("dve", lambda e: e.tensor_copy(out=out, in_=in_), reads=[in_], writes=[out])

    def vmemset(self, out, val):
        self.op("dve", lambda e: e.memset(out, val), writes=[out])

    def dma(self, eng, out, in_, sem):
        self.op(eng, lambda e: e.dma_start(out=out, in_=in_), reads=[in_], writes=[out], dma_sem=sem)

    def wget(self):
        j = self.wnext
        self.wnext += 1
        hi = min(len(self.wlist), j + NWB - 1)
        while self.wissued < hi:
            i = self.wissued
            src, nk = self.wlist[i]
            b = i % NWB
            buf = self.WS[b]
            jj = i % self.nslab
            t_i = i // self.nslab
            scr = self.wscr[jj][:, 0:nk * SLABC].rearrange("p (k n) -> p k n", n=SLABC)
            spill_tile = jj % 2
            multi = SPILL and len(self.tiles) > 2
            if t_i == 0 or (multi and t_i == 1 and spill_tile == 1) or not multi:
                self.dma("pool", buf[:, 0:nk, :], src, self.wsem[b])
                if multi and t_i == spill_tile:
                    self.dma("sp", scr, buf[:, 0:nk, :], self.ssem[b])
            else:
                if i % self.nslab == 0 and t_i in (1, 2):
                    for sm in self.ssem:
                        v = self.dmacnt.get(id(sm), 0)
                        if v and self.known["sp"].get(id(sm), 0) < v:
                            self.streams["sp"].append(("wait", sm, v))
                            self.known["sp"][id(sm)] = v
                self.dma("sp", buf[:, 0:nk, :], scr, self.wsem[b])
            self.wissued += 1
        nk = self.wlist[j][1]
        return self.WS[j % NWB][:, 0:nk, :]

    def slab(self, w2d, k0, nk, c0):
        return (w2d[k0 * 128:(k0 + nk) * 128, c0:c0 + SLABC].rearrange("(kc p) n -> p kc n", p=128), nk)

    def plan_weights(self):
        wl = []
        for (_c0, _p, _s) in self.tiles:
            for stg in self.stages:
                if stg == "sg":
                    for q in range(8):
                        wl.append(self.slab(self.sg_w_in, 0, 16, 2048 + q * SLABC))
                    for q in range(8):
                        wl.append(self.slab(self.sg_w_in, 0, 16, q * SLABC))
                    for q in range(8):
                        wl.append(self.slab(self.sg_w_out, 0, 16, q * SLABC))
                elif stg == "hg":
                    for q in range(8):
                        wl.append(self.slab(self.hg_w_in, 0, 16, 2048 + q * SLABC))
                        wl.append(self.slab(self.hg_w_in, 0, 16, q * SLABC))
                    for q in range(8):
                        wl.append(self.slab(self.hg_w_in, 0, 16, 4096 + q * SLABC))
                    for q in range(8):
                        wl.append(self.slab(self.hg_w_in, 0, 16, 6144 + q * SLABC))
                    for q in range(8):
                        wl.append(self.slab(self.hg_w_out, 0, 16, q * SLABC))
                else:
                    l, i = int(stg[3]), int(stg[4])
                    wu = self.w_up[l][i]
                    wd = self.w_down[l][i]
                    for s in range(22):
                        wl.append(self.slab(wu, 0, 16, s * SLABC))
                        wl.append(self.slab(wu, 0, 16, DFF + s * SLABC))
                    for q in range(8):
                        for (k0, nk) in ((0, 16), (16, 16), (32, 12)):
                            wl.append(self.slab(wd, k0, nk, q * SLABC))
        self.wlist = wl
        self.nslab = len(wl) // len(self.tiles)
        half = (self.nslab + 1) // 2 if SPILL else 1
        self.wscr_parts = [self.nc.dram_tensor(f"wscr{i}", [half, 128, NCH * SLABC], BF16).ap() for i in range(2)]
        self.wscr = [self.wscr_parts[(jj // half) % 2][jj % half] for jj in range(self.nslab)]
        self.wnext = 0
        self.wissued = 0

    def gbank(self):
        b = self.gb % 4
        self.gb += 1
        return self.PS[:, b, :]

    def qslot(self):
        i = self.qs % 8
        self.qs += 1
        return self.PS[:, 4 + i // 4, (i % 4) * 128:(i % 4) * 128 + 128]

    def rstd_from(self, R, ss, n):
        self.ts(R, ss, 1.0 / n, EPS, ALU.mult, ALU.add)
        self.op("act", lambda e: e.sqrt(R, R), reads=[R], writes=[R])
        self.op("dve", lambda e: e.reciprocal(out=R, in_=R), reads=[R], writes=[R])

    def prenorm(self, gi):
        NT = self.NT
        ss = self.PS[:, 6, 0:NT]
        for c in range(NCH):
            sq = self.SQ[c % 4][:, 0:NT]
            self.act(sq, self.X[:, c, 0:NT], AF.Square)
            self.mm(ss, self.ONES[:, :], sq, c == 0, c == NCH - 1, signal=True)
        R = self.R[0][:, 0:NT]
        self.rstd_from(R, ss, D)
        for c in range(NCH):
            self.stt(self.XN[:, c, 0:NT], self.X[:, c, 0:NT], self.G[:, gi * 16 + c:gi * 16 + c + 1], R,
                     ALU.mult, ALU.mult)

    def evac_y(self, py, c):
        NT = self.NT
        self.acopy(self.Y[:, c, 0:NT], py)
        sq = self.SQ[c % 4][:, 0:NT]
        self.act(sq, py, AF.Square)
        self.mm(self.PS[:, 7, 0:NT], self.ONES[:, :], sq, c == 0, c == NCH - 1, signal=True)

    def postnorm(self, Gt, gi):
        NT = self.NT
        R = self.R[1][:, 0:NT]
        self.rstd_from(R, self.PS[:, 7, 0:NT], D)
        for c in range(NCH):
            T = self.T[c % 2][:, 0:NT]
            self.stt(T, self.Y[:, c, 0:NT], Gt[:, gi * 16 + c:gi * 16 + c + 1], R, ALU.mult, ALU.mult)
            self.tt(self.X[:, c, 0:NT], self.X[:, c, 0:NT], T, ALU.add, eng=("pool" if (SPILL and self.ti > 1) else "dve"))

    def proj_out(self, src):
        NT = self.NT
        for q in range(8):
            w = self.wget()
            for m in range(2):
                py = self.gbank()[:, 0:NT]
                for kc in range(NCH):
                    self.mm(py, w[:, kc, m * 128:(m + 1) * 128], src[:, kc, 0:NT], kc == 0, kc == NCH - 1)
                self.evac_y(py, 2 * q + m)

    def ffn(self, l, i):
        NT = self.NT
        self.prenorm((l * 6 + (0 if i == 0 else 4)))
        for s in range(22):
            wa = self.wget()
            wb = self.wget()
            for m in range(2):
                j = 2 * s + m
                pa = self.gbank()[:, 0:NT]
                pb = self.gbank()[:, 0:NT]
                for kc in range(NCH):
                    self.mm(pa, wa[:, kc, m * 128:(m + 1) * 128], self.XN[:, kc, 0:NT], kc == 0, kc == NCH - 1)
                for kc in range(NCH):
                    self.mm(pb, wb[:, kc, m * 128:(m + 1) * 128], self.XN[:, kc, 0:NT], kc == 0, kc == NCH - 1)
                sa = self.SA[j % 2][:, 0:NT]
                self.act(sa, pa, AF.Silu)
                self.tt(self.HID[:, j, 0:NT], sa, pb, ALU.mult)
        for q in range(8):
            py = [self.gbank()[:, 0:NT] for _ in range(2)]
            for kp, (k0, nk) in enumerate(((0, 16), (16, 16), (32, 12))):
                w = self.wget()
                for m in range(2):
                    for kc in range(nk):
                        self.mm(py[m], w[:, kc, m * 128:(m + 1) * 128], self.HID[:, k0 + kc, 0:NT],
                                kp == 0 and kc == 0, kp == 2 and kc == nk - 1)
            for m in range(2):
                self.evac_y(py[m], 2 * q + m)
        self.postnorm(self.GH, l * 6 + (1 if i == 0 else 5))

    def tile_blocks(self):
        P, S = self.P, self.S
        bl = [(n * 128, 128) for n in range(P // 128)]
        if S:
            bl.append((P, S))
        return bl

    def sg_mixer(self):
        NT = self.NT
        blocks = self.tile_blocks()
        self.prenorm(2)
        LNG = self.Yflat[:, 0:2048]
        LNB = self.Yflat[:, 2048:4096]
        self.dma("sp", LNG, self.sg_ln_g.partition_broadcast(128), self.csem[0])
        self.dma("sp", LNB, self.sg_ln_b.partition_broadcast(128), self.csem[1])
        UC = [self.Yflat[:, 4096 + k * NTM:4096 + (k + 1) * NTM] for k in range(2)]
        VTMP = self.MIXf[:, 0:4 * 2048].rearrange("p (b d) -> p b d", d=2048)
        VT = self.MIXf[:, 8192:8192 + 4096].bitcast(BF16).rearrange("p (b d) -> p b d", d=2048)
        YB = self.MIXB[:, :].bitcast(BF16).rearrange("p (c t) -> p c t", t=NTM)
        for q in range(8):
            w = self.wget()
            for bi, (c0, wd) in enumerate(blocks):
                pv = self.gbank()[0:wd, 0:SLABC]
                for kc in range(NCH):
                    self.mm(pv, self.XN[:, kc, c0:c0 + wd], w[:, kc, :], kc == 0, kc == NCH - 1)
                self.act(VTMP[0:wd, bi, q * SLABC:(q + 1) * SLABC], pv, AF.Gelu)
        import os
        sgstop = int(os.environ.get("SGSTOP", "9"))
        if sgstop < 2:
            return
        for bi, (c0, wd) in enumerate(blocks):
            v = VTMP[0:wd, bi, :]
            st = self.STAT[0:wd, 0:24]
            for k in range(4):
                vk = VTMP[0:wd, bi, k * 512:(k + 1) * 512]
                sk = self.STAT[0:wd, k * 6:(k + 1) * 6]
                self.op("dve", lambda e, sk=sk, vk=vk: e.bn_stats(out=sk, in_=vk), reads=[vk], writes=[sk])
            mv = self.STAT[0:wd, 24:26]
            self.op("dve", lambda e, mv=mv, st=st: e.bn_aggr(out=mv, in_=st), reads=[st], writes=[mv])
            rs = self.STAT[0:wd, 26:27]
            nm = self.STAT[0:wd, 27:28]
            self.rstd_from(rs, self.STAT[0:wd, 25:26], 1.0)
            self.stt(nm, self.STAT[0:wd, 24:25], -1.0, rs, ALU.mult, ALU.mult)
            self.ts(v, v, rs, nm, ALU.mult, ALU.add)
            self.tt(v, v, LNG[0:wd, :], ALU.mult)
            if wd == 128:
                self.tt(VT[0:wd, bi, :], v, LNB[0:wd, :], ALU.add)
            else:
                self.tt(v, v, LNB[0:wd, :], ALU.add)
                self.dma("sp", self.o_sgv, v, self.osem[1])
                self.vcopy(VT[0:wd, bi, :], v)
        if sgstop < 3:
            return
        for q in range(8):
            w = self.wget()
            for m in range(2):
                c = 2 * q + m
                pu = self.gbank()[:, 0:NT]
                for kc in range(NCH):
                    self.mm(pu, w[:, kc, m * 128:(m + 1) * 128], self.XN[:, kc, 0:NT], kc == 0, kc == NCH - 1)
                uc = UC[c % 2][:, 0:NT]
                self.act(uc, pu, AF.Gelu)
                if sgstop < 4:
                    continue
                psc = self.gbank()[:, 0:NT]
                for bi, (c0, wd) in enumerate(blocks):
                    ps = psc[:, c0:c0 + wd]
                    self.mm(ps, VT[0:wd, bi, c * 128:(c + 1) * 128], self.WST[0:wd, c, 0:wd], True, False)
                    self.mm(ps, self.ONES[:, :], self.BS[:, c * 128:c * 128 + wd], False, True,
                            signal=(bi == len(blocks) - 1))
                self.tt(YB[:, c, 0:NT], uc, psc, ALU.mult)
        if sgstop < 5:
            return
        self.proj_out(YB)
        self.postnorm(self.G, 3)

    def hg_mixer(self, last):
        NT, P, S = self.NT, self.P, self.S
        blocks = self.tile_blocks()
        nblk = len(blocks)
        nch = P // 128
        self.prenorm(6 + 2)
        QT = self.MIXf[:, 0:3072].bitcast(BF16).rearrange("p (c t) -> p c t", t=NTM)
        KL = self.MIXf[:, 3072:6144].bitcast(BF16).rearrange("p (c t) -> p c t", t=NTM)
        VTOK = self.MIXf[:, 6144:6144 + 4096].bitcast(BF16).rearrange("p (b d) -> p b d", d=2048)
        SS = self.MIXf[:, 10240:10240 + 2048].rearrange("p (h v) -> p h v", v=128)
        OGB = self.MIXB[:, :].bitcast(BF16).rearrange("p (c t) -> p c t", t=NTM)
        TMPS = [self.MIXB[:, k * NTM:(k + 1) * NTM] for k in range(8)]
        if S:
            self.dma("sp", SS, self.s0, self.csem[2])
            for c in range(NCH):
                self.vmemset(self.E63[:, c, nch:nch + 1], 1.0)
        for q in range(8):
            wf = self.wget()
            wq = self.wget()
            for m in range(2):
                c = 2 * q + m
                t1, t2, t3, t4 = TMPS[(c % 2) * 4:(c % 2) * 4 + 4]
                t1, t2, t3, t4 = t1[:, 0:NT], t2[:, 0:NT], t3[:, 0:NT], t4[:, 0:NT]
                pf = self.gbank()[:, 0:NT]
                for kc in range(NCH):
                    self.mm(pf, wf[:, kc, m * 128:(m + 1) * 128], self.XN[:, kc, 0:NT], kc == 0, kc == NCH - 1)
                pq = self.gbank()[:, 0:NT]
                for kc in range(NCH):
                    self.mm(pq, wq[:, kc, m * 128:(m + 1) * 128], self.XN[:, kc, 0:NT], kc == 0, kc == NCH - 1)
                self.act(t1, pf, AF.Sigmoid)
                self.ts(t1, t1, self.OML[:, c:c + 1], self.LB[:, c:c + 1], ALU.mult, ALU.add)
                self.act(t2, t1, AF.Ln)
                self.ts(t1, t1, -1.0, 1.0, ALU.mult, ALU.add)
                self.op("dve", lambda e, t3=t3, t2=t2: e.tensor_tensor_scan(
                    out=t3, data0=self.RESET[:, 0:NT], data1=t2, initial=0.0, op0=ALU.mult, op1=ALU.add),
                    reads=[t2, self.RESET[:, 0:NT]], writes=[t3])
                b3 = t3[:, 0:P].rearrange("p (n j) -> p n j", j=128)
                bm3 = t2[:, 0:P].rearrange("p (n j) -> p n j", j=128)
                self.act(self.E63[:, c, 0:nch], b3[:, :, 63], AF.Exp)
                self.tt(bm3, b3, b3[:, :, 63:64].broadcast_to([128, nch, 128]), ALU.subtract)
                if S:
                    self.vcopy(t2[:, P:NT], t3[:, P:NT])
                self.act(t3, t2, AF.Exp)
                self.act(t2, t2, AF.Exp, scale=-1.0)
                e3 = t3[:, 0:P].rearrange("p (n j) -> p n j", j=128)
                self.vcopy(self.E2[:, c, 0:nch], e3[:, :, 127])
                if S:
                    self.vcopy(self.E2[:, c, nch:nch + 1], t3[:, NT - 1:NT])
                self.tt(KL[:, c, 0:NT], t1, t2, ALU.mult)
                self.act(t4, pq, AF.Silu)
                self.tt(QT[:, c, 0:NT], t4, t3, ALU.mult)
        for q in range(8):
            w = self.wget()
            for bi, (c0, wd) in enumerate(blocks):
                pv = self.gbank()[0:wd, 0:SLABC]
                for kc in range(NCH):
                    self.mm(pv, self.XN[:, kc, c0:c0 + wd], w[:, kc, :], kc == 0, kc == NCH - 1)
                self.acopy(VTOK[0:wd, bi, q * SLABC:(q + 1) * SLABC], pv)
        O = self.Y
        its = [(bi, c0, wd, c) for bi, (c0, wd) in enumerate(blocks) for c in range(NCH)]

        def views(i):
            bi, c0, wd, c = its[i]
            k = i % 2
            return dict(bi=bi, c0=c0, wd=wd, c=c, k=k, St=(SS if wd != 128 else self.S_),
                        Z=self.Zf[k], ZB=self.Zb[k], TM=self.Tm[k], KTT=self.KTT[k], SM=self.SMk[k],
                        q=QT[:, c, c0:c0 + wd], kk=KL[:, c, c0:c0 + wd],
                        v=VTOK[0:wd, bi, c * 128:(c + 1) * 128])

        def stage_a(i):
            d = views(i)
            wd, k, c, bi = d["wd"], d["k"], d["c"], d["bi"]
            psS = self.PS[:, k, 0:128]
            self.mm(psS[0:wd, 0:wd], d["kk"], d["q"], True, True)
            psT = self.PS[:, 2 + k, 0:64].bitcast(BF16)
            self.op("pe", lambda e, o=psT[0:wd, :], i_=d["kk"]: e.transpose(out=o, in_=i_, identity=self.IDB[:, :]),
                    reads=[d["kk"], self.IDB[:, :]], writes=[psT[0:wd, :]])
            self.tt(d["SM"][0:wd, 0:wd], psS[0:wd, 0:wd], self.MASKT[0:wd, 0:wd], ALU.mult)
            self.acopy(d["KTT"][0:wd, :], psT[0:wd, :])
            self.amul(d["Z"][:, :], d["St"][:, c, :], self.E63[:, c, bi:bi + 1])
            self.vcopy(d["ZB"][:, :], d["Z"][:, :])

        def stage_b(i):
            d = views(i)
            wd, k, c, bi, c0 = d["wd"], d["k"], d["c"], d["bi"], d["c0"]
            psO = self.PS[:, 4 + k, 0:128]
            self.mm(psO[:, 0:wd], d["ZB"][:, :], d["q"], True, False)
            self.mm(psO[:, 0:wd], d["v"], d["SM"][0:wd, 0:wd], False, True)
            psD = self.PS[:, 6 + k, 0:128]
            self.mm(psD[:, :], d["KTT"][0:wd, :], d["v"], True, True)
            self.acopy(O[:, c, c0:c0 + wd], psO[:, 0:wd])
            self.tt(d["TM"][:, :], d["Z"][:, :], psD[:, :], ALU.add)
            self.amul(d["St"][:, c, :], d["TM"][:, :], self.E2[:, c, bi:bi + 1])

        n_it = len(its)
        stage_a(0)
        for i in range(n_it):
            if i + 1 < n_it:
                stage_a(i + 1)
            stage_b(i)
        if last:
            self.dma("sp", self.o_sp, self.S_[:, :, :], self.osem[2])
            if S:
                self.dma("sp", self.o_ss, SS, self.osem[3])
        for q in range(8):
            w = self.wget()
            for m in range(2):
                c = 2 * q + m
                pg = self.gbank()[:, 0:NT]
                for kc in range(NCH):
                    self.mm(pg, w[:, kc, m * 128:(m + 1) * 128], self.XN[:, kc, 0:NT], kc == 0, kc == NCH - 1)
                sa = self.SA[c % 2][:, 0:NT]
                self.act(sa, pg, AF.Silu)
                sq = self.SQ[c % 4][:, 0:NT]
                self.act(sq, O[:, c, 0:NT], AF.Square)
                pss = self.gbank()[:, 0:NT]
                self.mm(pss, self.ONES[:, :], sq, True, True)
                R = self.R[c % 2][:, 0:NT]
                self.rstd_from(R, pss, 128)
                T = self.T[c % 2][:, 0:NT]
                self.stt(T, O[:, c, 0:NT], self.HGN[:, 0:1], R, ALU.mult, ALU.mult)
                self.tt(OGB[:, c, 0:NT], T, sa, ALU.mult)
        self.proj_out(OGB)
        self.postnorm(self.G, 6 + 3)

    def build(self):
        nc = self.nc
        dt = nc.dram_tensor
        self.xT = dt("xT", [128, NCH, TOT], F32, kind="ExternalInput").ap()
        self.s0 = dt("s0", [128, NCH, 128], F32, kind="ExternalInput").ap()
        gT = dt("gT", [128, 192], F32, kind="ExternalInput").ap()
        hgl = dt("hgl", [128, 32], F32, kind="ExternalInput").ap()
        hgn = dt("hgn", [128, 1], F32, kind="ExternalInput").ap()
        self.sg_ln_g = dt("sg_ln_g", [1, D], F32, kind="ExternalInput").ap()
        self.sg_ln_b = dt("sg_ln_b", [1, D], F32, kind="ExternalInput").ap()
        wst = dt("wst", [128, NCH, 128], F32, kind="ExternalInput").ap()
        bs = dt("bs", [1, D], F32, kind="ExternalInput").ap()
        has_ffn = any(st.startswith("ffn") for st in self.stages)
        has_sg = "sg" in self.stages
        has_hg = "hg" in self.stages
        self.dbg_shapes = {}
        def wshape(name, full, used):
            shp = full if used else [1] * (len(full) - 1) + [128]
            self.dbg_shapes[name] = shp
            return shp
        w_up = dt("ffn_w_up", wshape("ffn_w_up", [4, D, 2 * DFF], has_ffn), F32, kind="ExternalInput").ap()
        w_down = dt("ffn_w_down", wshape("ffn_w_down", [4, DFF, D], has_ffn), F32, kind="ExternalInput").ap()
        if has_ffn:
            self.w_up = [[w_up[l * 2 + i] for i in range(2)] for l in range(2)]
            self.w_down = [[w_down[l * 2 + i] for i in range(2)] for l in range(2)]
        self.sg_w_in = dt("sg_w_in", wshape("sg_w_in", [D, 2 * D], has_sg), F32, kind="ExternalInput").ap()
        self.sg_w_out = dt("sg_w_out", wshape("sg_w_out", [D, D], has_sg), F32, kind="ExternalInput").ap()
        self.hg_w_in = dt("hg_w_in", wshape("hg_w_in", [D, 4 * D], has_hg), F32, kind="ExternalInput").ap()
        self.hg_w_out = dt("hg_w_out", wshape("hg_w_out", [D, D], has_hg), F32, kind="ExternalInput").ap()
        self.o_y = dt("yT", [128, NCH, OWN + NSAMP], F32, kind="ExternalOutput").ap()
        self.o_sp = dt("o_sp", [128, NCH, 128], F32, kind="ExternalOutput").ap()
        self.o_ss = dt("o_ss", [128, NCH, 128], F32, kind="ExternalOutput").ap()
        self.o_sgv = dt("o_sgv", [NSAMP, D], F32, kind="ExternalOutput").ap()
        self.plan_weights()
        from contextlib import ExitStack
        with ExitStack() as es:
            sb = lambda name, shape, dtype: es.enter_context(nc.sbuf_tensor(name, shape, dtype))
            self.X = sb("X", [128, NCH, NTM], F32)
            self.XN = sb("XN", [128, NCH, NTM], BF16)
            self.Yflat = sb("Y", [128, NCH * NTM], F32)
            self.Y = self.Yflat[:, :].rearrange("p (c t) -> p c t", t=NTM)
            self.PAD = sb("PAD", [128, 992], F32)
            self.MIXf = sb("MIX", [128, 12288], F32)
            self.MIXB = sb("MIXB", [128, 3072], F32)
            assert nc.lookup_mloc(self.MIXB).addr == 131072, nc.lookup_mloc(self.MIXB).addr
            self.HID = self.MIXf[:, 0:44 * NTM // 2].bitcast(BF16).rearrange("p (c t) -> p c t", t=NTM)
            self.WS = [sb(f"WS{i}", [128, NCH, SLABC], BF16) for i in range(NWB)]
            self.G = sb("G", [128, 192], F32)
            self.GH = sb("GH", [128, 192], F32)
            HGL = sb("HGL", [128, 32], F32)
            self.LB = sb("LB", [128, 16], F32)
            self.OML = sb("OML", [128, 16], F32)
            self.HGN = sb("HGN", [128, 1], F32)
            self.ONES = sb("ONES", [128, 128], BF16)
            self.IDF = sb("IDF", [128, 128], F32)
            self.IDB = sb("IDB", [128, 128], BF16)
            self.MASKF = sb("MASKF", [128, 128], F32)
            self.MASKT = sb("MASKT", [128, 128], F32)
            self.RESET = sb("RESET", [128, NTM], F32)
            self.R = [sb(f"R{i}", [128, NTM], F32) for i in range(2)]
            self.T = [sb(f"T{i}", [128, NTM], F32) for i in range(2)]
            self.SA = [sb(f"SA{i}", [128, NTM], F32) for i in range(2)]
            self.SQ = [sb(f"SQ{i}", [128, NTM], BF16) for i in range(4)]
            self.STAT = sb("STAT", [128, 32], F32)
            self.S_ = sb("S", [128, NCH, 128], F32)
            self.E63 = sb("E63", [128, NCH, 4], F32)
            self.E2 = sb("E2", [128, NCH, 4], F32)
            self.WST = sb("WST", [128, NCH, 128], BF16)
            self.BS = sb("BS", [128, D], BF16)
            self.BSH = sb("BSH", [64, D], BF16)
            self.WSTF = self.Yflat[:, 0:2048].rearrange("p (g i) -> p g i", i=128)
            self.BSF = self.Yflat[0:64, 2048:4096]
            self.Zf = [sb(f"Zf{i}", [128, 128], F32) for i in range(2)]
            self.Zb = [sb(f"Zb{i}", [128, 128], BF16) for i in range(2)]
            self.Tm = [sb(f"Tm{i}", [128, 128], F32) for i in range(2)]
            self.KTT = [sb(f"KTT{i}", [128, 128], BF16) for i in range(2)]
            self.SMk = [sb(f"SM{i}", [128, 128], BF16) for i in range(2)]
            self.PS = es.enter_context(nc.psum_tensor("PS", [128, 8, 512], F32))
            sem = lambda name: es.enter_context(nc.semaphore(name))
            for e in ENGS:
                self.sems[e] = sem("s_" + e)
            self.wsem = [sem(f"w{i}") for i in range(NWB)]
            self.ssem = [sem(f"ws{i}") for i in range(NWB)]
            self.csem = [sem(f"c{i}") for i in range(8)]
            self.osem = [sem(f"o{i}") for i in range(4)]
            self.xsem = [sem(f"xld{i}") for i in range(NCH)]
            self.ysem = [sem(f"yst{i}") for i in range(NCH)]
            self.gb = 0
            self.qs = 0

            self.dma("sp", self.G[:, :], gT, self.csem[3])
            self.dma("sp", HGL[:, :], hgl, self.csem[4])
            self.dma("sp", self.HGN[:, :], hgn, self.csem[5])
            self.dma("sp", self.WSTF, wst, self.csem[6])
            self.dma("sp", self.BSF[0:1, :], bs, self.csem[7])
            self.dma("sp", self.BSF[32:33, :], bs, self.csem[0])
            self.ts(self.GH[:, :], self.G[:, :], 0.5, 0.0, ALU.mult, ALU.add)
            self.tt(self.LB[:, :], HGL[:, 16:32], HGL[:, 0:16], ALU.subtract)
            self.act(self.LB[:, :], self.LB[:, :], AF.Sigmoid)
            self.ts(self.OML[:, :], self.LB[:, :], -1.0, 1.0, ALU.mult, ALU.add)
            self.vmemset(self.ONES[:, :], 1.0)
            self.vmemset(self.S_[:, :, :], 0.0)
            self.vmemset(self.RESET[:, :], 1.0)
            self.vmemset(self.RESET[:, :].rearrange("p (n j) -> p n j", j=128)[:, :, 0:1], 0.0)
            self.op("pool", lambda e: e.memset(self.IDF[:, :], 0.0), writes=[self.IDF[:, :]])
            self.op("pool", lambda e: e.affine_select(out=self.IDF[:, :], in_=self.IDF[:, :], compare_op=ALU.not_equal,
                                                      fill=1.0, base=0, pattern=[[-1, 128]], channel_multiplier=1),
                    reads=[self.IDF[:, :]], writes=[self.IDF[:, :]])
            self.op("pool", lambda e: e.memset(self.MASKF[:, :], 1.0), writes=[self.MASKF[:, :]])
            self.op("pool", lambda e: e.affine_select(out=self.MASKT[:, :], in_=self.MASKF[:, :], compare_op=ALU.is_ge,
                                                      fill=0.0, base=0, pattern=[[1, 128]], channel_multiplier=-1),
                    reads=[self.MASKF[:, :]], writes=[self.MASKT[:, :]])
            self.vcopy(self.IDB[:, :], self.IDF[:, :])
            self.vcopy(self.WST[:, :, :], self.WSTF)
            self.vmemset(self.WST[64:128, :, 0:64], 0.0)
            self.vmemset(self.BS[:, :], 0.0)
            self.vcopy(self.BS[0:1, :], self.BSF[0:1, :])
            self.vcopy(self.BSH[32:33, :], self.BSF[32:33, :])
            self.tt(self.BSF[32:33, :], self.BSF[32:33, :], self.BSH[32:33, :], ALU.subtract)
            self.vcopy(self.BS[32:33, :], self.BSF[32:33, :])

            ntiles = len(self.tiles)
            for ti, (c0, P, S) in enumerate(self.tiles):
                self.P, self.S, self.NT = P, S, P + S
                NT = self.NT
                self.ti = ti
                for c in range(NCH):
                    self.dma("sp", self.X[:, c, 0:NT], self.xT[:, c, c0:c0 + NT], self.xsem[c])
                for stg in self.stages:
                    if stg == "sg":
                        self.sg_mixer()
                    elif stg == "hg":
                        self.hg_mixer(ti == ntiles - 1)
                    else:
                        self.ffn(int(stg[3]), int(stg[4]))
                lo = HALO if ti == 0 else 0
                for c in range(NCH):
                    self.dma("sp", self.o_y[:, c, c0 + lo - HALO:c0 + NT - HALO], self.X[:, c, lo:NT], self.ysem[c])
            for s in self.osem + self.ysem:
                v = self.dmacnt.get(id(s), 0)
                if v:
                    self.streams["sp"].append(("wait", s, v))

            with nc.Block() as block:
                @block.tensor
                def _(e):
                    self.replay("pe", e)

                @block.scalar
                def _(e):
                    self.replay("act", e)

                @block.vector
                def _(e):
                    self.replay("dve", e)

                @block.gpsimd
                def _(e):
                    self.replay("pool", e)

                @block.sync
                def _(e):
                    self.replay("sp", e)
        return nc


def _layout_inputs(inp):
    f = lambda a: np.ascontiguousarray(a, dtype=np.float32)
    xp = inp["x_prompt"][0]
    xs = inp["x_sample"]
    st = inp["state_hgrn"][0]
    shared = {
        "gT": f(inp["norm_g"].reshape(12, NCH, 128).transpose(2, 0, 1).reshape(128, 192)),
        "hgl": f(inp["hg_lower"].reshape(2, NCH, 128).transpose(2, 0, 1).reshape(128, 32)),
        "hgn": f(inp["hg_norm_g"].reshape(128, 1)),
        "sg_ln_g": f(inp["sg_ln_g"].reshape(1, D)),
        "sg_ln_b": f(inp["sg_ln_b"].reshape(1, D)),
        "wst": f(inp["sg_w_s"][0].transpose(2, 0, 1)),
        "bs": f(inp["sg_b_s"].reshape(1, D)),
        "ffn_w_up": f(inp["ffn_w_up"].reshape(4, D, 2 * DFF)),
        "ffn_w_down": f(inp["ffn_w_down"].reshape(4, DFF, D)),
        "sg_w_in": f(inp["sg_w_in"][0]),
        "sg_w_out": f(inp["sg_w_out"][0]),
        "hg_w_in": f(inp["hg_w_in"][0]),
        "hg_w_out": f(inp["hg_w_out"][0]),
    }
    maps = []
    for c in range(8):
        halo = np.zeros((HALO, D), np.float32) if c == 0 else xp[c * OWN - HALO:c * OWN]
        rows = np.concatenate([halo, xp[c * OWN:(c + 1) * OWN], xs[c]], axis=0)
        m = dict(shared)
        m["xT"] = f(rows.reshape(TOT, NCH, 128).transpose(2, 1, 0))
        m["s0"] = f(st[c].transpose(1, 0, 2))
        maps.append(m)
    return maps


_NC_CACHE = {}


def kernel(**inputs):
    inp = {k: np.asarray(v) for k, v in inputs.items()}
    if "nc" not in _NC_CACHE:
        nc = bass.Bass("TRN2", target_bir_lowering=False)
        Builder(nc).build()
        _NC_CACHE["nc"] = nc
    nc = _NC_CACHE["nc"]
    maps = _layout_inputs(inp)
    res = run_bass_kernel_spmd(nc, maps, core_ids=list(range(8)))
    y_prompt = np.empty((1, 8 * OWN, D), np.float32)
    y_sample = np.empty((8, NSAMP, D), np.float32)
    st_p = np.empty((1, 1, 16, 128, 128), np.float32)
    st_s = np.empty((1, 8, 16, 128, 128), np.float32)
    sgv = np.empty((1, 8, NSAMP, D), np.float32)
    for c in range(8):
        r = res.results[c]
        yT = np.asarray(r["yT"])
        y_prompt[0, c * OWN:(c + 1) * OWN] = yT[:, :, :OWN].transpose(2, 1, 0).reshape(OWN, D)
        y_sample[c] = yT[:, :, OWN:].transpose(2, 1, 0).reshape(NSAMP, D)
        st_s[0, c] = np.asarray(r["o_ss"]).transpose(1, 0, 2)
        sgv[0, c] = np.asarray(r["o_sgv"])
        if c == 7:
            st_p[0, 0] = np.asarray(r["o_sp"]).transpose(1, 0, 2)
    return (y_prompt, y_sample, st_p, st_s, sgv)
```

```python
import numpy as np
import concourse.bass as bass
import concourse.mybir as mybir
from concourse.bass_utils import run_bass_kernel_spmd

F32 = mybir.dt.float32
BF16 = mybir.dt.bfloat16
AF = mybir.ActivationFunctionType
ALU = mybir.AluOpType

D = 2048
DFF = 5632
NCH = 16
EPS = 1e-6
HALO = 128
OWN = 2048
NSAMP = 16
TOT = HALO + OWN + NSAMP
NTM = 384
TILES = [(0, 384, 0), (384, 384, 0), (768, 384, 0), (1152, 384, 0), (1536, 384, 0), (1920, 256, 16)]
NWB = 5
SLABC = 256
BLK = 256
ENGS = ("pe", "act", "dve", "pool", "sp")


def _esz(dt):
    return 4 if dt == F32 else 2


class Builder:
    def __init__(self, nc, n_tiles=None, stages=None):
        self.nc = nc
        self.stages = stages or ["ffn00", "sg", "ffn01", "ffn10", "hg", "ffn11"]
        self.streams = {e: [] for e in ENGS}
        self.cnt = {e: 0 for e in ENGS}
        self.known = {e: {} for e in ENGS}
        self.lastw = {}
        self.readers = {}
        self.sems = {}
        self.dmacnt = {}
        self.tiles = TILES if n_tiles is None else TILES[:n_tiles]

    def blocks(self, ap):
        t = ap.tensor
        if type(t).__name__.startswith("DRam"):
            return ()
        es = _esz(t.dtype)
        name = t.name
        dims = [(s_, c_) for s_, c_ in ap.ap[1:] if s_ != 0 and c_ != 1]
        inner = 1
        if dims and dims[-1][0] == 1:
            inner = dims[-1][1]
            dims = dims[:-1]
        offs = [ap.offset]
        for s_, c_ in dims:
            offs = [o + i * s_ for o in offs for i in range(c_)]
        out = set()
        for o in offs:
            b0 = (o * es) // BLK
            b1 = ((o + inner) * es - 1) // BLK
            for b in range(b0, b1 + 1):
                out.add((name, b))
        return out

    def op(self, eng, fn, reads=(), writes=(), signal=True, dma_sem=None):
        deps = {}

        def add(tok):
            s, v = tok
            if deps.get(s, 0) < v:
                deps[s] = v
        rb = set()
        for a in reads:
            rb |= set(self.blocks(a))
        wb = set()
        for a in writes:
            wb |= set(self.blocks(a))
        for k in rb:
            t = self.lastw.get(k)
            if t is not None:
                add(t)
        for k in wb:
            t = self.lastw.get(k)
            if t is not None:
                add(t)
            for s, v in self.readers.get(k, {}).items():
                add((s, v))
        my = self.sems[eng]
        for s, v in deps.items():
            if s is my and (eng in ("pe", "sp") or dma_sem is not None):
                continue
            if s is my and dma_sem is None and v > self.cnt[eng]:
                continue
            if self.known[eng].get(id(s), 0) >= v:
                continue
            self.known[eng][id(s)] = v
            self.streams[eng].append(("wait", s, v))
        if dma_sem is not None:
            self.dmacnt[id(dma_sem)] = self.dmacnt.get(id(dma_sem), 0) + 16
            tok = (dma_sem, self.dmacnt[id(dma_sem)])
            self.streams[eng].append(("op", fn, dma_sem, 16))
        elif signal:
            self.cnt[eng] += 1
            tok = (my, self.cnt[eng])
            self.streams[eng].append(("op", fn, my, 1))
        else:
            tok = (my, self.cnt[eng] + 1)
            self.streams[eng].append(("op", fn, None, 0))
        for k in rb:
            d = self.readers.setdefault(k, {})
            if d.get(tok[0], 0) < tok[1]:
                d[tok[0]] = tok[1]
        for k in wb:
            self.lastw[k] = tok
            self.readers[k] = {}
        return tok

    def replay(self, eng, e):
        for item in self.streams[eng]:
            if item[0] == "wait":
                e.wait_ge(item[1], item[2])
            else:
                ins = item[1](e)
                if item[2] is not None:
                    ins.then_inc(item[2], item[3])

    def mm(self, out, lhsT, rhs, start, stop, signal=None):
        if signal is None:
            signal = stop
        self.op("pe", lambda e: e.matmul(out, lhsT, rhs, start=start, stop=stop),
                reads=[lhsT, rhs], writes=[out], signal=signal)

    def act(self, out, in_, func, scale=None, bias=None):
        kw = {}
        if scale is not None:
            kw["scale"] = scale
        if bias is not None:
            kw["bias"] = bias
        rd = [in_] + [x for x in (scale, bias) if x is not None and not isinstance(x, float)]
        self.op("act", lambda e: e.activation(out=out, in_=in_, func=func, **kw), reads=rd, writes=[out])

    def amul(self, out, in_, mul):
        rd = [in_] + ([] if isinstance(mul, float) else [mul])
        self.op("act", lambda e: e.mul(out, in_, mul), reads=rd, writes=[out])

    def acopy(self, out, in_):
        self.op("act", lambda e: e.copy(out, in_), reads=[in_], writes=[out])

    def tt(self, out, in0, in1, op, eng="dve"):
        self.op(eng, lambda e: e.tensor_tensor(out=out, in0=in0, in1=in1, op=op), reads=[in0, in1], writes=[out])

    def ts(self, out, in0, s1, s2, op0, op1=None, eng="dve"):
        rd = [in0] + [x for x in (s1, s2) if x is not None and not isinstance(x, float)]
        if op1 is None:
            self.op(eng, lambda e: e.tensor_scalar(out=out, in0=in0, scalar1=s1, scalar2=None, op0=op0),
                    reads=rd, writes=[out])
        else:
            self.op(eng, lambda e: e.tensor_scalar(out=out, in0=in0, scalar1=s1, scalar2=s2, op0=op0, op1=op1),
                    reads=rd, writes=[out])

    def stt(self, out, in0, scalar, in1, op0, op1):
        rd = [in0, in1] + ([] if isinstance(scalar, float) else [scalar])
        self.op("dve", lambda e: e.scalar_tensor_tensor(out=out, in0=in0, scalar=scalar, in1=in1, op0=op0, op1=op1),
                reads=rd, writes=[out])

    def vcopy(self, out, in_):
        self.op("dve", lambda e: e.tensor_copy(out=out, in_=in_), reads=[in_], writes=[out])

    def vmemset(self, out, val):
        self.op("dve", lambda e: e.memset(out, val), writes=[out])

    def dma(self, eng, out, in_, sem):
        self.op(eng, lambda e: e.dma_start(out=out, in_=in_), reads=[in_], writes=[out], dma_sem=sem)

    def wget(self):
        j = self.wnext
        self.wnext += 1
        hi = min(len(self.wlist), j + NWB - 1)
        while self.wissued < hi:
            i = self.wissued
            src, nk = self.wlist[i]
            b = i % NWB
            buf = self.WS[b]
            jj = i % self.nslab
            scr = self.wscr[jj][:, 0:nk * SLABC].rearrange("p (k n) -> p k n", n=SLABC)
            if i < self.nslab:
                self.dma("pool", buf[:, 0:nk, :], src, self.wsem[b])
                if len(self.tiles) > 1:
                    self.dma("sp", scr, buf[:, 0:nk, :], self.ssem[b])
            else:
                if i == self.nslab:
                    for sm in self.ssem:
                        v = self.dmacnt.get(id(sm), 0)
                        if v:
                            self.streams["sp"].append(("wait", sm, v))
                            self.known["sp"][id(sm)] = v
                self.dma("sp", buf[:, 0:nk, :], scr, self.wsem[b])
            self.wissued += 1
        nk = self.wlist[j][1]
        return self.WS[j % NWB][:, 0:nk, :]

    def slab(self, w2d, k0, nk, c0):
        return (w2d[k0 * 128:(k0 + nk) * 128, c0:c0 + SLABC].rearrange("(kc p) n -> p kc n", p=128), nk)

    def plan_weights(self):
        wl = []
        for (_c0, _p, _s) in self.tiles:
            for stg in self.stages:
                if stg == "sg":
                    for q in range(8):
                        wl.append(self.slab(self.sg_w_in, 0, 16, 2048 + q * SLABC))
                    for q in range(8):
                        wl.append(self.slab(self.sg_w_in, 0, 16, q * SLABC))
                    for q in range(8):
                        wl.append(self.slab(self.sg_w_out, 0, 16, q * SLABC))
                elif stg == "hg":
                    for q in range(8):
                        wl.append(self.slab(self.hg_w_in, 0, 16, 2048 + q * SLABC))
                        wl.append(self.slab(self.hg_w_in, 0, 16, q * SLABC))
                    for q in range(8):
                        wl.append(self.slab(self.hg_w_in, 0, 16, 4096 + q * SLABC))
                    for q in range(8):
                        wl.append(self.slab(self.hg_w_in, 0, 16, 6144 + q * SLABC))
                    for q in range(8):
                        wl.append(self.slab(self.hg_w_out, 0, 16, q * SLABC))
                else:
                    l, i = int(stg[3]), int(stg[4])
                    wu = self.w_up[l][i]
                    wd = self.w_down[l][i]
                    for s in range(22):
                        wl.append(self.slab(wu, 0, 16, s * SLABC))
                        wl.append(self.slab(wu, 0, 16, DFF + s * SLABC))
                    for q in range(8):
                        for (k0, nk) in ((0, 16), (16, 16), (32, 12)):
                            wl.append(self.slab(wd, k0, nk, q * SLABC))
        self.wlist = wl
        self.nslab = len(wl) // len(self.tiles)
        half = (self.nslab + 1) // 2
        self.wscr_parts = [self.nc.dram_tensor(f"wscr{i}", [half, 128, NCH * SLABC], BF16).ap() for i in range(2)]
        self.wscr = [self.wscr_parts[jj // half][jj % half] for jj in range(self.nslab)]
        self.wnext = 0
        self.wissued = 0

    def gbank(self):
        b = self.gb % 4
        self.gb += 1
        return self.PS[:, b, :]

    def qslot(self):
        i = self.qs % 8
        self.qs += 1
        return self.PS[:, 4 + i // 4, (i % 4) * 128:(i % 4) * 128 + 128]

    def rstd_from(self, R, ss, n):
        self.ts(R, ss, 1.0 / n, EPS, ALU.mult, ALU.add)
        self.act(R, R, AF.Ln)
        self.act(R, R, AF.Exp, scale=-0.5)

    def recip(self, out, in_):
        self.op("dve", lambda e: e.reciprocal(out=out, in_=in_), reads=[in_], writes=[out])

    def sig_from_psum(self, buf, ps):
        self.act(buf, ps, AF.Exp, scale=-1.0)
        self.ts(buf, buf, 1.0, 1.0, ALU.add, ALU.mult)
        self.recip(buf, buf)

    def prenorm(self, gi):
        NT = self.NT
        ss = self.PS[:, 6, 0:NT]
        for c in range(NCH):
            sq = self.SQ[c % 4][:, 0:NT]
            self.act(sq, self.X[:, c, 0:NT], AF.Square)
            self.mm(ss, self.ONES[:, :], sq, c == 0, c == NCH - 1, signal=True)
        R = self.R[0][:, 0:NT]
        self.rstd_from(R, ss, D)
        for c in range(NCH):
            self.stt(self.XN[:, c, 0:NT], self.X[:, c, 0:NT], self.G[:, gi * 16 + c:gi * 16 + c + 1], R,
                     ALU.mult, ALU.mult)

    def evac_y(self, py, c):
        NT = self.NT
        self.acopy(self.Y[:, c, 0:NT], py)
        sq = self.SQ[c % 4][:, 0:NT]
        self.act(sq, py, AF.Square)
        self.mm(self.PS[:, 7, 0:NT], self.ONES[:, :], sq, c == 0, c == NCH - 1, signal=True)

    def postnorm(self, Gt, gi):
        NT = self.NT
        R = self.R[1][:, 0:NT]
        self.rstd_from(R, self.PS[:, 7, 0:NT], D)
        for c in range(NCH):
            T = self.T[c % 2][:, 0:NT]
            self.stt(T, self.Y[:, c, 0:NT], Gt[:, gi * 16 + c:gi * 16 + c + 1], R, ALU.mult, ALU.mult)
            self.tt(self.X[:, c, 0:NT], self.X[:, c, 0:NT], T, ALU.add, eng=("pool" if self.ti > 0 else "dve"))

    def proj_out(self, src):
        NT = self.NT
        for q in range(8):
            w = self.wget()
            for m in range(2):
                py = self.gbank()[:, 0:NT]
                for kc in range(NCH):
                    self.mm(py, w[:, kc, m * 128:(m + 1) * 128], src[:, kc, 0:NT], kc == 0, kc == NCH - 1)
                self.evac_y(py, 2 * q + m)

    def ffn(self, l, i):
        NT = self.NT
        self.prenorm((l * 6 + (0 if i == 0 else 4)))
        for s in range(22):
            wa = self.wget()
            wb = self.wget()
            for m in range(2):
                j = 2 * s + m
                pa = self.gbank()[:, 0:NT]
                pb = self.gbank()[:, 0:NT]
                for kc in range(NCH):
                    self.mm(pa, wa[:, kc, m * 128:(m + 1) * 128], self.XN[:, kc, 0:NT], kc == 0, kc == NCH - 1)
                for kc in range(NCH):
                    self.mm(pb, wb[:, kc, m * 128:(m + 1) * 128], self.XN[:, kc, 0:NT], kc == 0, kc == NCH - 1)
                sa = self.SA[j % 2][:, 0:NT]
                self.sig_from_psum(sa, pa)
                self.tt(sa, sa, pa, ALU.mult)
                self.tt(self.HID[:, j, 0:NT], sa, pb, ALU.mult)
        for q in range(8):
            py = [self.gbank()[:, 0:NT] for _ in range(2)]
            for kp, (k0, nk) in enumerate(((0, 16), (16, 16), (32, 12))):
                w = self.wget()
                for m in range(2):
                    for kc in range(nk):
                        self.mm(py[m], w[:, kc, m * 128:(m + 1) * 128], self.HID[:, k0 + kc, 0:NT],
                                kp == 0 and kc == 0, kp == 2 and kc == nk - 1)
            for m in range(2):
                self.evac_y(py[m], 2 * q + m)
        self.postnorm(self.GH, l * 6 + (1 if i == 0 else 5))

    def tile_blocks(self):
        P, S = self.P, self.S
        bl = [(n * 128, 128) for n in range(P // 128)]
        if S:
            bl.append((P, S))
        return bl

    def sg_mixer(self):
        NT = self.NT
        blocks = self.tile_blocks()
        self.prenorm(2)
        LNG = self.Yflat[:, 0:2048]
        LNB = self.Yflat[:, 2048:4096]
        self.dma("sp", LNG, self.sg_ln_g.partition_broadcast(128), self.csem[0])
        self.dma("sp", LNB, self.sg_ln_b.partition_broadcast(128), self.csem[1])
        UC = [self.Yflat[:, 4096 + k * NTM:4096 + (k + 1) * NTM] for k in range(2)]
        VTMP = self.MIXf[:, 0:4 * 2048].rearrange("p (b d) -> p b d", d=2048)
        VT = self.MIXf[:, 8192:8192 + 4096].bitcast(BF16).rearrange("p (b d) -> p b d", d=2048)
        YB = self.MIXB[:, :].bitcast(BF16).rearrange("p (c t) -> p c t", t=NTM)
        for q in range(8):
            w = self.wget()
            for bi, (c0, wd) in enumerate(blocks):
                pv = self.gbank()[0:wd, 0:SLABC]
                for kc in range(NCH):
                    self.mm(pv, self.XN[:, kc, c0:c0 + wd], w[:, kc, :], kc == 0, kc == NCH - 1)
                self.act(VTMP[0:wd, bi, q * SLABC:(q + 1) * SLABC], pv, AF.Gelu)
        import os
        sgstop = int(os.environ.get("SGSTOP", "9"))
        if sgstop < 2:
            return
        for bi, (c0, wd) in enumerate(blocks):
            v = VTMP[0:wd, bi, :]
            st = self.STAT[0:wd, 0:24]
            for k in range(4):
                vk = VTMP[0:wd, bi, k * 512:(k + 1) * 512]
                sk = self.STAT[0:wd, k * 6:(k + 1) * 6]
                self.op("dve", lambda e, sk=sk, vk=vk: e.bn_stats(out=sk, in_=vk), reads=[vk], writes=[sk])
            mv = self.STAT[0:wd, 24:26]
            self.op("dve", lambda e, mv=mv, st=st: e.bn_aggr(out=mv, in_=st), reads=[st], writes=[mv])
            rs = self.STAT[0:wd, 26:27]
            nm = self.STAT[0:wd, 27:28]
            self.rstd_from(rs, self.STAT[0:wd, 25:26], 1.0)
            self.stt(nm, self.STAT[0:wd, 24:25], -1.0, rs, ALU.mult, ALU.mult)
            self.ts(v, v, rs, nm, ALU.mult, ALU.add)
            self.tt(v, v, LNG[0:wd, :], ALU.mult)
            if wd == 128:
                self.tt(VT[0:wd, bi, :], v, LNB[0:wd, :], ALU.add)
            else:
                self.tt(v, v, LNB[0:wd, :], ALU.add)
                self.dma("sp", self.o_sgv, v, self.osem[1])
                self.vcopy(VT[0:wd, bi, :], v)
        if sgstop < 3:
            return
        for q in range(8):
            w = self.wget()
            for m in range(2):
                c = 2 * q + m
                pu = self.gbank()[:, 0:NT]
                for kc in range(NCH):
                    self.mm(pu, w[:, kc, m * 128:(m + 1) * 128], self.XN[:, kc, 0:NT], kc == 0, kc == NCH - 1)
                uc = UC[c % 2][:, 0:NT]
                self.act(uc, pu, AF.Gelu)
                if sgstop < 4:
                    continue
                psc = self.gbank()[:, 0:NT]
                for bi, (c0, wd) in enumerate(blocks):
                    ps = psc[:, c0:c0 + wd]
                    self.mm(ps, VT[0:wd, bi, c * 128:(c + 1) * 128], self.WST[0:wd, c, 0:wd], True, False)
                    self.mm(ps, self.ONES[:, :], self.BS[:, c * 128:c * 128 + wd], False, True,
                            signal=(bi == len(blocks) - 1))
                self.tt(YB[:, c, 0:NT], uc, psc, ALU.mult)
        if sgstop < 5:
            return
        self.proj_out(YB)
        self.postnorm(self.G, 3)

    def hg_mixer(self, last):
        NT, P, S = self.NT, self.P, self.S
        blocks = self.tile_blocks()
        nblk = len(blocks)
        nch = P // 128
        self.prenorm(6 + 2)
        QT = self.MIXf[:, 0:3072].bitcast(BF16).rearrange("p (c t) -> p c t", t=NTM)
        KL = self.MIXf[:, 3072:6144].bitcast(BF16).rearrange("p (c t) -> p c t", t=NTM)
        VTOK = self.MIXf[:, 6144:6144 + 4096].bitcast(BF16).rearrange("p (b d) -> p b d", d=2048)
        SS = self.MIXf[:, 10240:10240 + 2048].rearrange("p (h v) -> p h v", v=128)
        OGB = self.MIXB[:, :].bitcast(BF16).rearrange("p (c t) -> p c t", t=NTM)
        TMPS = [self.MIXB[:, k * NTM:(k + 1) * NTM] for k in range(8)]
        if S:
            self.dma("sp", SS, self.s0, self.csem[2])
            for c in range(NCH):
                self.vmemset(self.E63[:, c, nch:nch + 1], 1.0)
        for q in range(8):
            wf = self.wget()
            wq = self.wget()
            for m in range(2):
                c = 2 * q + m
                t1, t2, t3, t4 = TMPS[(c % 2) * 4:(c % 2) * 4 + 4]
                t1, t2, t3, t4 = t1[:, 0:NT], t2[:, 0:NT], t3[:, 0:NT], t4[:, 0:NT]
                pf = self.gbank()[:, 0:NT]
                for kc in range(NCH):
                    self.mm(pf, wf[:, kc, m * 128:(m + 1) * 128], self.XN[:, kc, 0:NT], kc == 0, kc == NCH - 1)
                pq = self.gbank()[:, 0:NT]
                for kc in range(NCH):
                    self.mm(pq, wq[:, kc, m * 128:(m + 1) * 128], self.XN[:, kc, 0:NT], kc == 0, kc == NCH - 1)
                self.sig_from_psum(t1, pf)
                self.ts(t1, t1, self.OML[:, c:c + 1], self.LB[:, c:c + 1], ALU.mult, ALU.add)
                self.act(t2, t1, AF.Ln)
                self.ts(t1, t1, -1.0, 1.0, ALU.mult, ALU.add)
                self.op("dve", lambda e, t3=t3, t2=t2: e.tensor_tensor_scan(
                    out=t3, data0=self.RESET[:, 0:NT], data1=t2, initial=0.0, op0=ALU.mult, op1=ALU.add),
                    reads=[t2, self.RESET[:, 0:NT]], writes=[t3])
                b3 = t3[:, 0:P].rearrange("p (n j) -> p n j", j=128)
                bm3 = t2[:, 0:P].rearrange("p (n j) -> p n j", j=128)
                self.act(self.E63[:, c, 0:nch], b3[:, :, 63], AF.Exp)
                self.tt(bm3, b3, b3[:, :, 63:64].broadcast_to([128, nch, 128]), ALU.subtract)
                if S:
                    self.vcopy(t2[:, P:NT], t3[:, P:NT])
                self.act(t3, t2, AF.Exp)
                self.act(t2, t2, AF.Exp, scale=-1.0)
                e3 = t3[:, 0:P].rearrange("p (n j) -> p n j", j=128)
                self.vcopy(self.E2[:, c, 0:nch], e3[:, :, 127])
                if S:
                    self.vcopy(self.E2[:, c, nch:nch + 1], t3[:, NT - 1:NT])
                pe_ = "pool" if self.ti > 0 else "dve"
                self.tt(KL[:, c, 0:NT], t1, t2, ALU.mult, eng=pe_)
                self.sig_from_psum(t4, pq)
                self.tt(t4, t4, pq, ALU.mult)
                self.tt(QT[:, c, 0:NT], t4, t3, ALU.mult, eng=pe_)
        for q in range(8):
            w = self.wget()
            for bi, (c0, wd) in enumerate(blocks):
                pv = self.gbank()[0:wd, 0:SLABC]
                for kc in range(NCH):
                    self.mm(pv, self.XN[:, kc, c0:c0 + wd], w[:, kc, :], kc == 0, kc == NCH - 1)
                self.acopy(VTOK[0:wd, bi, q * SLABC:(q + 1) * SLABC], pv)
        O = self.Y
        its = [(bi, c0, wd, c) for bi, (c0, wd) in enumerate(blocks) for c in range(NCH)]

        def views(i):
            bi, c0, wd, c = its[i]
            k = i % 2
            return dict(bi=bi, c0=c0, wd=wd, c=c, k=k, St=(SS if wd != 128 else self.S_),
                        Z=self.Zf[k], ZB=self.Zb[k], TM=self.Tm[k], KTT=self.KTT[k], SM=self.SMk[k],
                        q=QT[:, c, c0:c0 + wd], kk=KL[:, c, c0:c0 + wd],
                        v=VTOK[0:wd, bi, c * 128:(c + 1) * 128])

        def stage_a(i):
            d = views(i)
            wd, k, c, bi = d["wd"], d["k"], d["c"], d["bi"]
            psS = self.PS[:, k, 0:128]
            self.mm(psS[0:wd, 0:wd], d["kk"], d["q"], True, True)
            psT = self.PS[:, 2 + k, 0:64].bitcast(BF16)
            self.op("pe", lambda e, o=psT[0:wd, :], i_=d["kk"]: e.transpose(out=o, in_=i_, identity=self.IDB[:, :]),
                    reads=[d["kk"], self.IDB[:, :]], writes=[psT[0:wd, :]])
            self.tt(d["SM"][0:wd, 0:wd], psS[0:wd, 0:wd], self.MASKT[0:wd, 0:wd], ALU.mult)
            self.acopy(d["KTT"][0:wd, :], psT[0:wd, :])
            self.amul(d["Z"][:, :], d["St"][:, c, :], self.E63[:, c, bi:bi + 1])
            self.vcopy(d["ZB"][:, :], d["Z"][:, :])

        def stage_b(i):
            d = views(i)
            wd, k, c, bi, c0 = d["wd"], d["k"], d["c"], d["bi"], d["c0"]
            psO = self.PS[:, 4 + k, 0:128]
            self.mm(psO[:, 0:wd], d["ZB"][:, :], d["q"], True, False)
            self.mm(psO[:, 0:wd], d["v"], d["SM"][0:wd, 0:wd], False, True)
            psD = self.PS[:, 6 + k, 0:128]
            self.mm(psD[:, :], d["KTT"][0:wd, :], d["v"], True, True)
            self.acopy(O[:, c, c0:c0 + wd], psO[:, 0:wd])
            self.tt(d["TM"][:, :], d["Z"][:, :], psD[:, :], ALU.add)
            self.amul(d["St"][:, c, :], d["TM"][:, :], self.E2[:, c, bi:bi + 1])

        n_it = len(its)
        stage_a(0)
        for i in range(n_it):
            if i + 1 < n_it:
                stage_a(i + 1)
            stage_b(i)
        if last:
            self.dma("sp", self.o_sp, self.S_[:, :, :], self.osem[2])
            if S:
                self.dma("sp", self.o_ss, SS, self.osem[3])
        for q in range(8):
            w = self.wget()
            for m in range(2):
                c = 2 * q + m
                pg = self.gbank()[:, 0:NT]
                for kc in range(NCH):
                    self.mm(pg, w[:, kc, m * 128:(m + 1) * 128], self.XN[:, kc, 0:NT], kc == 0, kc == NCH - 1)
                sa = self.SA[c % 2][:, 0:NT]
                self.sig_from_psum(sa, pg)
                self.tt(sa, sa, pg, ALU.mult)
                sq = self.SQ[c % 4][:, 0:NT]
                self.act(sq, O[:, c, 0:NT], AF.Square)
                pss = self.gbank()[:, 0:NT]
                self.mm(pss, self.ONES[:, :], sq, True, True)
                R = self.R[c % 2][:, 0:NT]
                self.rstd_from(R, pss, 128)
                T = self.T[c % 2][:, 0:NT]
                self.stt(T, O[:, c, 0:NT], self.HGN[:, 0:1], R, ALU.mult, ALU.mult)
                self.tt(OGB[:, c, 0:NT], T, sa, ALU.mult)
        self.proj_out(OGB)
        self.postnorm(self.G, 6 + 3)

    def build(self):
        nc = self.nc
        dt = nc.dram_tensor
        self.xT = dt("xT", [128, NCH, TOT], F32, kind="ExternalInput").ap()
        self.s0 = dt("s0", [128, NCH, 128], F32, kind="ExternalInput").ap()
        gT = dt("gT", [128, 192], F32, kind="ExternalInput").ap()
        hgl = dt("hgl", [128, 32], F32, kind="ExternalInput").ap()
        hgn = dt("hgn", [128, 1], F32, kind="ExternalInput").ap()
        self.sg_ln_g = dt("sg_ln_g", [1, D], F32, kind="ExternalInput").ap()
        self.sg_ln_b = dt("sg_ln_b", [1, D], F32, kind="ExternalInput").ap()
        wst = dt("wst", [128, NCH, 128], F32, kind="ExternalInput").ap()
        bs = dt("bs", [1, D], F32, kind="ExternalInput").ap()
        has_ffn = any(st.startswith("ffn") for st in self.stages)
        has_sg = "sg" in self.stages
        has_hg = "hg" in self.stages
        self.dbg_shapes = {}
        def wshape(name, full, used):
            shp = full if used else [1] * (len(full) - 1) + [128]
            self.dbg_shapes[name] = shp
            return shp
        w_up = dt("ffn_w_up", wshape("ffn_w_up", [4, D, 2 * DFF], has_ffn), F32, kind="ExternalInput").ap()
        w_down = dt("ffn_w_down", wshape("ffn_w_down", [4, DFF, D], has_ffn), F32, kind="ExternalInput").ap()
        if has_ffn:
            self.w_up = [[w_up[l * 2 + i] for i in range(2)] for l in range(2)]
            self.w_down = [[w_down[l * 2 + i] for i in range(2)] for l in range(2)]
        self.sg_w_in = dt("sg_w_in", wshape("sg_w_in", [D, 2 * D], has_sg), F32, kind="ExternalInput").ap()
        self.sg_w_out = dt("sg_w_out", wshape("sg_w_out", [D, D], has_sg), F32, kind="ExternalInput").ap()
        self.hg_w_in = dt("hg_w_in", wshape("hg_w_in", [D, 4 * D], has_hg), F32, kind="ExternalInput").ap()
        self.hg_w_out = dt("hg_w_out", wshape("hg_w_out", [D, D], has_hg), F32, kind="ExternalInput").ap()
        self.o_y = dt("yT", [128, NCH, OWN + NSAMP], F32, kind="ExternalOutput").ap()
        self.o_sp = dt("o_sp", [128, NCH, 128], F32, kind="ExternalOutput").ap()
        self.o_ss = dt("o_ss", [128, NCH, 128], F32, kind="ExternalOutput").ap()
        self.o_sgv = dt("o_sgv", [NSAMP, D], F32, kind="ExternalOutput").ap()
        self.plan_weights()
        from contextlib import ExitStack
        with ExitStack() as es:
            sb = lambda name, shape, dtype: es.enter_context(nc.sbuf_tensor(name, shape, dtype))
            self.X = sb("X", [128, NCH, NTM], F32)
            self.XN = sb("XN", [128, NCH, NTM], BF16)
            self.Yflat = sb("Y", [128, NCH * NTM], F32)
            self.Y = self.Yflat[:, :].rearrange("p (c t) -> p c t", t=NTM)
            self.PAD = sb("PAD", [128, 992], F32)
            self.MIXf = sb("MIX", [128, 12288], F32)
            self.MIXB = sb("MIXB", [128, 3072], F32)
            assert nc.lookup_mloc(self.MIXB).addr == 131072, nc.lookup_mloc(self.MIXB).addr
            self.HID = self.MIXf[:, 0:44 * NTM // 2].bitcast(BF16).rearrange("p (c t) -> p c t", t=NTM)
            self.WS = [sb(f"WS{i}", [128, NCH, SLABC], BF16) for i in range(NWB)]
            self.G = sb("G", [128, 192], F32)
            self.GH = sb("GH", [128, 192], F32)
            HGL = sb("HGL", [128, 32], F32)
            self.LB = sb("LB", [128, 16], F32)
            self.OML = sb("OML", [128, 16], F32)
            self.HGN = sb("HGN", [128, 1], F32)
            self.ONES = sb("ONES", [128, 128], BF16)
            self.IDF = sb("IDF", [128, 128], F32)
            self.IDB = sb("IDB", [128, 128], BF16)
            self.MASKF = sb("MASKF", [128, 128], F32)
            self.MASKT = sb("MASKT", [128, 128], F32)
            self.RESET = sb("RESET", [128, NTM], F32)
            self.R = [sb(f"R{i}", [128, NTM], F32) for i in range(2)]
            self.T = [sb(f"T{i}", [128, NTM], F32) for i in range(2)]
            self.SA = [sb(f"SA{i}", [128, NTM], F32) for i in range(2)]
            self.SQ = [sb(f"SQ{i}", [128, NTM], BF16) for i in range(4)]
            self.STAT = sb("STAT", [128, 32], F32)
            self.S_ = sb("S", [128, NCH, 128], F32)
            self.E63 = sb("E63", [128, NCH, 4], F32)
            self.E2 = sb("E2", [128, NCH, 4], F32)
            self.WST = sb("WST", [128, NCH, 128], BF16)
            self.BS = sb("BS", [128, D], BF16)
            self.BSH = sb("BSH", [64, D], BF16)
            self.WSTF = self.Yflat[:, 0:2048].rearrange("p (g i) -> p g i", i=128)
            self.BSF = self.Yflat[0:64, 2048:4096]
            self.Zf = [sb(f"Zf{i}", [128, 128], F32) for i in range(2)]
            self.Zb = [sb(f"Zb{i}", [128, 128], BF16) for i in range(2)]
            self.Tm = [sb(f"Tm{i}", [128, 128], F32) for i in range(2)]
            self.KTT = [sb(f"KTT{i}", [128, 128], BF16) for i in range(2)]
            self.SMk = [sb(f"SM{i}", [128, 128], BF16) for i in range(2)]
            self.PS = es.enter_context(nc.psum_tensor("PS", [128, 8, 512], F32))
            sem = lambda name: es.enter_context(nc.semaphore(name))
            for e in ENGS:
                self.sems[e] = sem("s_" + e)
            self.wsem = [sem(f"w{i}") for i in range(NWB)]
            self.ssem = [sem(f"ws{i}") for i in range(NWB)]
            self.csem = [sem(f"c{i}") for i in range(8)]
            self.osem = [sem(f"o{i}") for i in range(4)]
            self.xsem = [sem(f"xld{i}") for i in range(NCH)]
            self.ysem = [sem(f"yst{i}") for i in range(NCH)]
            self.gb = 0
            self.qs = 0

            self.dma("sp", self.G[:, :], gT, self.csem[3])
            self.dma("sp", HGL[:, :], hgl, self.csem[4])
            self.dma("sp", self.HGN[:, :], hgn, self.csem[5])
            self.dma("sp", self.WSTF, wst, self.csem[6])
            self.dma("sp", self.BSF[0:1, :], bs, self.csem[7])
            self.dma("sp", self.BSF[32:33, :], bs, self.csem[0])
            self.ts(self.GH[:, :], self.G[:, :], 0.5, 0.0, ALU.mult, ALU.add)
            self.tt(self.LB[:, :], HGL[:, 16:32], HGL[:, 0:16], ALU.subtract)
            self.act(self.LB[:, :], self.LB[:, :], AF.Sigmoid)
            self.ts(self.OML[:, :], self.LB[:, :], -1.0, 1.0, ALU.mult, ALU.add)
            self.vmemset(self.ONES[:, :], 1.0)
            self.vmemset(self.S_[:, :, :], 0.0)
            self.vmemset(self.RESET[:, :], 1.0)
            self.vmemset(self.RESET[:, :].rearrange("p (n j) -> p n j", j=128)[:, :, 0:1], 0.0)
            self.op("pool", lambda e: e.memset(self.IDF[:, :], 0.0), writes=[self.IDF[:, :]])
            self.op("pool", lambda e: e.affine_select(out=self.IDF[:, :], in_=self.IDF[:, :], compare_op=ALU.not_equal,
                                                      fill=1.0, base=0, pattern=[[-1, 128]], channel_multiplier=1),
                    reads=[self.IDF[:, :]], writes=[self.IDF[:, :]])
            self.op("pool", lambda e: e.memset(self.MASKF[:, :], 1.0), writes=[self.MASKF[:, :]])
            self.op("pool", lambda e: e.affine_select(out=self.MASKT[:, :], in_=self.MASKF[:, :], compare_op=ALU.is_ge,
                                                      fill=0.0, base=0, pattern=[[1, 128]], channel_multiplier=-1),
                    reads=[self.MASKF[:, :]], writes=[self.MASKT[:, :]])
            self.vcopy(self.IDB[:, :], self.IDF[:, :])
            self.vcopy(self.WST[:, :, :], self.WSTF)
            self.vmemset(self.WST[64:128, :, 0:64], 0.0)
            self.vmemset(self.BS[:, :], 0.0)
            self.vcopy(self.BS[0:1, :], self.BSF[0:1, :])
            self.vcopy(self.BSH[32:33, :], self.BSF[32:33, :])
            self.tt(self.BSF[32:33, :], self.BSF[32:33, :], self.BSH[32:33, :], ALU.subtract)
            self.vcopy(self.BS[32:33, :], self.BSF[32:33, :])

            ntiles = len(self.tiles)
            for ti, (c0, P, S) in enumerate(self.tiles):
                self.P, self.S, self.NT = P, S, P + S
                NT = self.NT
                self.ti = ti
                for c in range(NCH):
                    self.dma("sp", self.X[:, c, 0:NT], self.xT[:, c, c0:c0 + NT], self.xsem[c])
                for stg in self.stages:
                    if stg == "sg":
                        self.sg_mixer()
                    elif stg == "hg":
                        self.hg_mixer(ti == ntiles - 1)
                    else:
                        self.ffn(int(stg[3]), int(stg[4]))
                lo = HALO if ti == 0 else 0
                for c in range(NCH):
                    self.dma("sp", self.o_y[:, c, c0 + lo - HALO:c0 + NT - HALO], self.X[:, c, lo:NT], self.ysem[c])
            for s in self.osem + self.ysem:
                v = self.dmacnt.get(id(s), 0)
                if v:
                    self.streams["sp"].append(("wait", s, v))

            with nc.Block() as block:
                @block.tensor
                def _(e):
                    self.replay("pe", e)

                @block.scalar
                def _(e):
                    self.replay("act", e)

                @block.vector
                def _(e):
                    self.replay("dve", e)

                @block.gpsimd
                def _(e):
                    self.replay("pool", e)

                @block.sync
                def _(e):
                    self.replay("sp", e)
        return nc


def _layout_inputs(inp):
    f = lambda a: np.ascontiguousarray(a, dtype=np.float32)
    xp = inp["x_prompt"][0]
    xs = inp["x_sample"]
    st = inp["state_hgrn"][0]
    shared = {
        "gT": f(inp["norm_g"].reshape(12, NCH, 128).transpose(2, 0, 1).reshape(128, 192)),
        "hgl": f(inp["hg_lower"].reshape(2, NCH, 128).transpose(2, 0, 1).reshape(128, 32)),
        "hgn": f(inp["hg_norm_g"].reshape(128, 1)),
        "sg_ln_g": f(inp["sg_ln_g"].reshape(1, D)),
        "sg_ln_b": f(inp["sg_ln_b"].reshape(1, D)),
        "wst": f(inp["sg_w_s"][0].transpose(2, 0, 1)),
        "bs": f(inp["sg_b_s"].reshape(1, D)),
        "ffn_w_up": f(inp["ffn_w_up"].reshape(4, D, 2 * DFF)),
        "ffn_w_down": f(inp["ffn_w_down"].reshape(4, DFF, D)),
        "sg_w_in": f(inp["sg_w_in"][0]),
        "sg_w_out": f(inp["sg_w_out"][0]),
        "hg_w_in": f(inp["hg_w_in"][0]),
        "hg_w_out": f(inp["hg_w_out"][0]),
    }
    maps = []
    for c in range(8):
        halo = np.zeros((HALO, D), np.float32) if c == 0 else xp[c * OWN - HALO:c * OWN]
        rows = np.concatenate([halo, xp[c * OWN:(c + 1) * OWN], xs[c]], axis=0)
        m = dict(shared)
        m["xT"] = f(rows.reshape(TOT, NCH, 128).transpose(2, 1, 0))
        m["s0"] = f(st[c].transpose(1, 0, 2))
        maps.append(m)
    return maps


_NC_CACHE = {}


def kernel(**inputs):
    inp = {k: np.asarray(v) for k, v in inputs.items()}
    if "nc" not in _NC_CACHE:
        nc = bass.Bass("TRN2", target_bir_lowering=False)
        Builder(nc).build()
        _NC_CACHE["nc"] = nc
    nc = _NC_CACHE["nc"]
    maps = _layout_inputs(inp)
    res = run_bass_kernel_spmd(nc, maps, core_ids=list(range(8)))
    y_prompt = np.empty((1, 8 * OWN, D), np.float32)
    y_sample = np.empty((8, NSAMP, D), np.float32)
    st_p = np.empty((1, 1, 16, 128, 128), np.float32)
    st_s = np.empty((1, 8, 16, 128, 128), np.float32)
    sgv = np.empty((1, 8, NSAMP, D), np.float32)
    for c in range(8):
        r = res.results[c]
        yT = np.asarray(r["yT"])
        y_prompt[0, c * OWN:(c + 1) * OWN] = yT[:, :, :OWN].transpose(2, 1, 0).reshape(OWN, D)
        y_sample[c] = yT[:, :, OWN:].transpose(2, 1, 0).reshape(NSAMP, D)
        st_s[0, c] = np.asarray(r["o_ss"]).transpose(1, 0, 2)
        sgv[0, c] = np.asarray(r["o_sgv"])
        if c == 7:
            st_p[0, 0] = np.asarray(r["o_sp"]).transpose(1, 0, 2)
    return (y_prompt, y_sample, st_p, st_s, sgv)
```
